# Optimizing a Trainium2 kernel written in Bass

```python
import jax, jax.numpy as jnp
from jax import lax
import numpy as np

D_MODEL = 2048
BATCH = 8
SEQ = 2048
DEPTH = 1

HEAD_DIM = 128
N_Q_HEADS = 8
N_KV_HEADS = 2
Q_GROUP = N_Q_HEADS // N_KV_HEADS
ATTN_WIDTH = N_Q_HEADS * HEAD_DIM
KV_WIDTH = N_KV_HEADS * HEAD_DIM
WINDOW = 128
BLOCK = 128
ROPE_THETA = 10000.0
CONV_WIDTH = D_MODEL // 2
CONV_K = 3
N_BRANCH = 2
N_MEM = 256
MEM_HEADS = 4
MEM_HEAD_DIM = 128
MEM_WIDTH = MEM_HEADS * MEM_HEAD_DIM
D_FF = -(-8 * D_MODEL // (3 * 256)) * 256
RMS_EPS = 1e-6
NEG_INF = -1e30

IN_SIZES = (ATTN_WIDTH, KV_WIDTH, KV_WIDTH, CONV_WIDTH, CONV_WIDTH, CONV_WIDTH, N_BRANCH * D_MODEL)
IN_WIDTH = sum(IN_SIZES)
IN_SPLITS = tuple(int(i) for i in np.cumsum(IN_SIZES)[:-1])

kernel_name = "hybrid_gated_swa_shortconv_encoder"


def rms_norm(t, g):
    tf = t.astype(jnp.float32)
    y = tf * lax.rsqrt(jnp.mean(tf * tf, axis=-1, keepdims=True) + RMS_EPS)
    return (y * g.astype(jnp.float32)).astype(t.dtype)


def rope_tables(s):
    inv = 1.0 / (ROPE_THETA ** (jnp.arange(0, HEAD_DIM, 2, dtype=jnp.float32) / HEAD_DIM))
    ang = jnp.arange(s, dtype=jnp.float32)[:, None] * inv[None, :]
    return jnp.cos(ang), jnp.sin(ang)


def apply_rope(t, cos, sin):
    half = HEAD_DIM // 2
    t1, t2 = t[..., :half], t[..., half:]
    c = cos[None, :, None, :].astype(t.dtype)
    s = sin[None, :, None, :].astype(t.dtype)
    return jnp.concatenate([t1 * c - t2 * s, t2 * c + t1 * s], axis=-1)


def windowed_gqa_sink(q, k, v, sink):
    b, s = q.shape[0], q.shape[1]
    nb = s // BLOCK
    scale = HEAD_DIM ** -0.5
    qb = q.reshape(b, nb, BLOCK, N_KV_HEADS, Q_GROUP, HEAD_DIM)

    def band(t):
        tb = t.reshape(b, nb, BLOCK, N_KV_HEADS, HEAD_DIM)
        tp = jnp.pad(tb, ((0, 0), (1, 1), (0, 0), (0, 0), (0, 0)))
        return jnp.concatenate([tp[:, :-2], tp[:, 1:-1], tp[:, 2:]], axis=2)

    kb, vb = band(k), band(v)
    q_pos = jnp.arange(BLOCK)[:, None]
    k_off = jnp.arange(3 * BLOCK)[None, :] - BLOCK
    in_window = jnp.abs(k_off - q_pos) <= WINDOW
    k_abs = jnp.arange(nb)[:, None] * BLOCK + k_off
    in_range = (k_abs >= 0) & (k_abs < s)
    valid = in_window[None] & in_range[:, None, :]

    scores = jnp.einsum('bnqhgd,bnkhd->bnhgqk', qb, kb).astype(jnp.float32) * scale
    scores = jnp.where(valid[None, :, None, None], scores, NEG_INF)
    sink_col = jnp.broadcast_to(
        sink.astype(jnp.float32).reshape(1, 1, N_KV_HEADS, Q_GROUP, 1, 1),
        scores.shape[:-1] + (1,))
    probs = jax.nn.softmax(jnp.concatenate([scores, sink_col], axis=-1), axis=-1)[..., :-1]
    out = jnp.einsum('bnhgqk,bnkhd->bnqhgd', probs.astype(v.dtype), vb)
    return out.reshape(b, s, ATTN_WIDTH)


def short_conv_centred(u, w):
    up = jnp.pad(u, ((0, 0), (1, 1), (0, 0)))
    return up[:, :-2] * w[0] + up[:, 1:-1] * w[1] + up[:, 2:] * w[2]


def memory_cross_attention(h, mem_n, w_cq, w_ckv, w_co):
    b, s = h.shape[0], h.shape[1]
    m = mem_n.shape[1]
    q = (h @ w_cq).reshape(b, s, MEM_HEADS, MEM_HEAD_DIM)
    k, v = jnp.split(mem_n @ w_ckv, 2, axis=-1)
    k = k.reshape(b, m, MEM_HEADS, MEM_HEAD_DIM)
    v = v.reshape(b, m, MEM_HEADS, MEM_HEAD_DIM)
    scores = jnp.einsum('bshd,bmhd->bhsm', q, k).astype(jnp.float32) * (MEM_HEAD_DIM ** -0.5)
    probs = jax.nn.softmax(scores, axis=-1).astype(v.dtype)
    out = jnp.einsum('bhsm,bmhd->bshd', probs, v).reshape(b, s, MEM_WIDTH)
    return out @ w_co


def setup_inputs(seed: int = 0) -> dict:
    key = jax.random.key(seed)
    ks = jax.random.split(key, 24)
    f32 = jnp.float32

    def w(k, shape, fan_in):
        return jax.random.normal(k, shape, f32) * (fan_in ** -0.5)

    def gain(k, shape):
        return 1.0 + 0.02 * jax.random.normal(k, shape, f32)

    L = DEPTH
    return {
        "x": jax.random.normal(ks[0], (BATCH, SEQ, D_MODEL), f32),
        "mem": jax.random.normal(ks[1], (BATCH, N_MEM, D_MODEL), f32),
        "g_mix": gain(ks[2], (L, D_MODEL)),
        "w_in": w(ks[3], (L, D_MODEL, IN_WIDTH), D_MODEL),
        "sink": 0.5 * jax.random.normal(ks[4], (L, N_Q_HEADS), f32),
        "conv_w": w(ks[5], (L, CONV_K, CONV_WIDTH), CONV_K),
        "b_gate": 0.1 * jax.random.normal(ks[6], (L, N_BRANCH * D_MODEL), f32),
        "w_attn_out": w(ks[7], (L, ATTN_WIDTH, D_MODEL), ATTN_WIDTH),
        "w_conv_out": w(ks[8], (L, CONV_WIDTH, D_MODEL), CONV_WIDTH),
        "w_o": w(ks[9], (L, D_MODEL, D_MODEL), D_MODEL),
        "g_cross": gain(ks[10], (L, D_MODEL)),
        "g_mem": gain(ks[11], (L, D_MODEL)),
        "w_cq": w(ks[12], (L, D_MODEL, MEM_WIDTH), D_MODEL),
        "w_ckv": w(ks[13], (L, D_MODEL, 2 * MEM_WIDTH), D_MODEL),
        "w_co": w(ks[14], (L, MEM_WIDTH, D_MODEL), MEM_WIDTH),
        "g_ffn": gain(ks[15], (L, D_MODEL)),
        "w_gate": w(ks[16], (L, D_MODEL, D_FF), D_MODEL),
        "w_up": w(ks[17], (L, D_MODEL, D_FF), D_MODEL),
        "w_down": w(ks[18], (L, D_FF, D_MODEL), D_FF),
        "g_final": gain(ks[19], (D_MODEL,)),
    }


def reference(x, mem, g_mix, w_in, sink, conv_w, b_gate, w_attn_out, w_conv_out, w_o,
              g_cross, g_mem, w_cq, w_ckv, w_co, g_ffn, w_gate, w_up, w_down, g_final):
    b, s = x.shape[0], x.shape[1]
    cos, sin = rope_tables(s)
    for l in range(DEPTH):
        h = rms_norm(x, g_mix[l])
        z = h @ w_in[l]
        q, k, v, cu, cb, cc, gl = jnp.split(z, IN_SPLITS, axis=-1)

        q = apply_rope(q.reshape(b, s, N_Q_HEADS, HEAD_DIM), cos, sin)
        k = apply_rope(k.reshape(b, s, N_KV_HEADS, HEAD_DIM), cos, sin)
        v = v.reshape(b, s, N_KV_HEADS, HEAD_DIM)
        y_attn = windowed_gqa_sink(q, k, v, sink[l]) @ w_attn_out[l]

        y_conv = (cb * short_conv_centred(cc * cu, conv_w[l])) @ w_conv_out[l]

        g_a, g_c = jnp.split(jax.nn.sigmoid(gl + b_gate[l]), 2, axis=-1)
        x = x + (g_a * y_attn + g_c * y_conv) @ w_o[l]

        x = x + memory_cross_attention(rms_norm(x, g_cross[l]), rms_norm(mem, g_mem[l]),
                                       w_cq[l], w_ckv[l], w_co[l])

        hf = rms_norm(x, g_ffn[l])
        x = x + (jax.nn.silu(hf @ w_gate[l]) * (hf @ w_up[l])) @ w_down[l]
    return rms_norm(x, g_final)
```

```python
import numpy as np
from contextlib import ExitStack
import concourse.bass as bass
import concourse.mybir as mybir
from concourse.bass_utils import run_bass_kernel_spmd

F32 = mybir.dt.float32
BF16 = mybir.dt.bfloat16
ALU = mybir.AluOpType
AF = mybir.ActivationFunctionType

P = 128
D = 2048
S = 2048
T = 512
NT = S // T
EXT = T + 256
NMEM = 256
DFF = 5632
NFB = DFF // 128
FG = 4
FGB = NFB // FG
EPS = 1e-6
SCALE = 128 ** -0.5
NBUF = 6
DT_SIZE = {F32: 4, BF16: 2}


class Op:
    __slots__ = ("eng", "fn", "deps", "is_dma", "need_inc", "incval", "dsem", "dval", "idx")


def ap_interval(ap):
    es = DT_SIZE[ap.dtype]
    pstride = ap.ap[0][0]
    off = ap.offset % pstride if pstride > 0 else ap.offset
    ext = 1
    for st, cnt in ap.ap[1:]:
        ext += (cnt - 1) * abs(st)
    return (ap.tensor.name, off * es, (off + ext) * es)


class Prog:
    def __init__(self, nc):
        self.nc = nc
        self.ops = []
        self.spaces = {}
        self.engs = {"pe": nc.tensor, "act": nc.scalar, "dve": nc.vector, "sp": nc.sync, "pool": nc.gpsimd}

    def add(self, eng, fn, reads=(), writes=(), is_dma=False):
        op = Op()
        op.eng = eng; op.fn = fn; op.is_dma = is_dma; op.need_inc = False
        op.incval = 0; op.dsem = None; op.dval = 0; op.idx = len(self.ops)
        deps = {}
        for ap in reads:
            if ap is None or ap.tensor.name not in self.spaces and False:
                continue
            name, lo, hi = ap_interval(ap)
            sp = self.spaces.setdefault(name, {"w": [], "r": []})
            for (wl, wh, wop) in sp["w"]:
                if wl < hi and lo < wh:
                    deps[wop.idx] = (wop, True)
        for ap in writes:
            name, lo, hi = ap_interval(ap)
            sp = self.spaces.setdefault(name, {"w": [], "r": []})
            for (wl, wh, wop) in sp["w"]:
                if wl < hi and lo < wh and wop.idx not in deps:
                    deps[wop.idx] = (wop, False)
            for (rl, rh, rop) in sp["r"]:
                if rl < hi and lo < rh and rop.idx not in deps and rop is not op:
                    deps[rop.idx] = (rop, False)
        for ap in writes:
            name, lo, hi = ap_interval(ap)
            sp = self.spaces[name]
            sp["w"] = [e for e in sp["w"] if not (lo <= e[0] and e[1] <= hi)]
            sp["r"] = [e for e in sp["r"] if not (lo <= e[0] and e[1] <= hi)]
            sp["w"].append((lo, hi, op))
        for ap in reads:
            name, lo, hi = ap_interval(ap)
            sp = self.spaces[name]
            if not is_dma:
                sp["r"] = [e for e in sp["r"] if not (e[0] == lo and e[1] == hi and e[2].eng == eng and not e[2].is_dma)]
            sp["r"].append((lo, hi, op))
        final = []
        for (dop, raw) in deps.values():
            if dop.is_dma:
                final.append(dop)
            elif dop.eng == eng and not is_dma:
                if eng == "pe":
                    continue
                final.append(dop)
            else:
                final.append(dop)
        for dop in final:
            if not dop.is_dma:
                dop.need_inc = True
        op.deps = final
        self.ops.append(op)
        return op

    def emit(self, es, final_waits):
        nc = self.nc
        csem = {e: es.enter_context(nc.semaphore("c_" + e)) for e in ("pe", "act", "dve")}
        npool = {"sp": 24, "pool": 12}
        dsems = {q: [es.enter_context(nc.semaphore(f"d_{q}{i}")) for i in range(n)] for q, n in npool.items()}
        duse = {q: [0] * n for q, n in npool.items()}
        dcount = {q: 0 for q in npool}
        cnt = {e: 0 for e in csem}
        for op in self.ops:
            if op.is_dma:
                q = op.eng
                i = dcount[q] % npool[q]
                dcount[q] += 1
                duse[q][i] += 1
                op.dsem = (q, i)
                op.dval = 16 * duse[q][i]
            elif op.need_inc:
                cnt[op.eng] += 1
                op.incval = cnt[op.eng]
        waited = {e: {} for e in self.engs}
        for op in self.ops:
            E = self.engs[op.eng]
            w = waited[op.eng]
            need = {}
            if op.is_dma and op.dval > 16:
                key = ("d",) + op.dsem
                need[key] = max(need.get(key, 0), op.dval - 16)
            for d in op.deps:
                if d.is_dma:
                    key = ("d",) + d.dsem
                    need[key] = max(need.get(key, 0), d.dval)
                else:
                    key = ("c", d.eng)
                    need[key] = max(need.get(key, 0), d.incval)
            for key, val in need.items():
                if w.get(key, 0) >= val:
                    continue
                w[key] = val
                sem = csem[key[1]] if key[0] == "c" else dsems[key[1]][key[2]]
                E.wait_ge(sem, val)
            ins = op.fn(E)
            if op.is_dma:
                ins.then_inc(dsems[op.dsem[0]][op.dsem[1]], 16)
            elif op.need_inc:
                ins.then_inc(csem[op.eng], 1)
        E = nc.sync
        need = {}
        for d in final_waits:
            key = d.dsem
            need[key] = max(need.get(key, 0), d.dval)
        for key, val in need.items():
            E.wait_ge(dsems[key[0]][key[1]], val)


def piece_order():
    lst = []
    for m in (8, 9):
        lst.append(("F", "w_in", m))
    lst.append(("Tv", 0)); lst.append(("Tv", 8))
    for m in range(8):
        lst.append(("F", "w_in", m))
    for j in range(8):
        lst.append(("F", "w_in", 12 + j)); lst.append(("F", "w_in", 28 + j)); lst.append(("F", "w_in", 20 + j))
    for c in range(16):
        lst.append(("F", "w_in", 36 + c)); lst.append(("F", "w_in", 52 + c)); lst.append(("AOCO", c))
    for cg in range(4):
        for kg in range(4):
            lst.append(("T", "w_o", 4 * kg, 4, cg))
    for h in range(4):
        lst.append(("F", "w_cq", h))
    for cg in range(4):
        lst.append(("T", "w_co", 0, 4, cg))
    for fg in range(FG):
        for jj in range(FGB):
            lst.append(("F", "w_gate", fg * FGB + jj)); lst.append(("F", "w_up", fg * FGB + jj))
        for cg in range(4):
            for q in range(3):
                nk = 4 if q < 2 else FGB - 8
                lst.append(("T", "w_down", fg * FGB + 4 * q, nk, cg))
    return lst


def mem_piece_order():
    lst = []
    for h in range(4):
        lst.append(("F", "w_ck", h))
    for kg in range(4):
        lst.append(("T", "w_cv", 4 * kg, 4, 0))
    return lst


N_MEMP = 8
N_TILEP = len(piece_order())


class _Stop(Exception):
    pass


def build_program(ntiles=NT, dbg=(), stop=None):
    nc = bass.Bass("TRN2", target_bir_lowering=False)
    x_d = nc.dram_tensor("x", [S, D], F32, kind="ExternalInput").ap()
    mem_d = nc.dram_tensor("mem", [NMEM, D], F32, kind="ExternalInput").ap()
    wp_d = nc.dram_tensor("wp", [N_MEMP + N_TILEP, P, 2048], F32, kind="ExternalInput").ap()
    gv_d = nc.dram_tensor("gv", [5, D], F32, kind="ExternalInput").ap()
    small_d = nc.dram_tensor("small", [P, 64], F32, kind="ExternalInput").ap()
    cs_d = nc.dram_tensor("cs", [2, P, S + 256], F32, kind="ExternalInput").ap()
    cst_d = nc.dram_tensor("cst", [P, 1280], F32, kind="ExternalInput").ap()
    out_d = nc.dram_tensor("out", [S, D], F32, kind="ExternalOutput").ap()
    dbg_d = {}
    for name, shape in dbg:
        dbg_d[name] = nc.dram_tensor("dbg_" + name, list(shape), F32, kind="ExternalOutput").ap()

    with ExitStack() as es:
        ARENA = 207 * 1024
        arena = es.enter_context(nc.sbuf_tensor("arena", [P, ARENA // 2], BF16))
        ps = es.enter_context(nc.psum_tensor("ps", [P, 4096], F32))
        pr = Prog(nc)
        cur = [0]

        def alloc(nbytes):
            lo = cur[0]
            cur[0] += (nbytes + 63) // 64 * 64
            assert cur[0] <= ARENA, cur[0]
            return lo

        def view(lo, dt, shape):
            n = int(np.prod(shape))
            es_ = DT_SIZE[dt]
            a = arena[:, lo // 2: lo // 2 + n * es_ // 2]
            if dt != BF16:
                a = a.bitcast(dt)
            if len(shape) == 2:
                a = a.rearrange("p (a b) -> p a b", b=shape[1])
            elif len(shape) == 3:
                a = a.rearrange("p (a b c) -> p a b c", b=shape[1], c=shape[2])
            return a

        def buf(dt, shape):
            return view(alloc(int(np.prod(shape)) * DT_SIZE[dt]), dt, shape)

        xres = buf(F32, (4, D))
        hT = buf(BF16, (16, EXT))
        xs = buf(BF16, (2, D))
        gB = buf(F32, (D,))
        xh = buf(F32, (2, D))
        cosT = buf(F32, (EXT,))
        sinT = buf(F32, (EXT,))
        wb = buf(BF16, (NBUF, 2048))
        cst = buf(BF16, (1280,))
        ident = cst[:, 0:128]
        ones = cst[:, 128:256]
        maskP = cst[:, 256:768]
        maskN = cst[:, 768:1280]
        small = buf(F32, (64,))
        convw = small[:, 0:24]
        bgate = small[:, 24:56]
        esink = small[:, 56:64]
        esB = buf(F32, (8, 128))
        stat = buf(F32, (32,))
        ssp = buf(F32, (16,))
        junk = buf(BF16, (512,))
        KmT = buf(BF16, (4, NMEM))
        Vm = buf(BF16, (2, 512))
        u0 = cur[0]
        qT = buf(BF16, (8, T))
        kT = buf(BF16, (2, EXT))
        Vt = buf(BF16, (6, 256))
        rtmp = buf(F32, (4, 512))
        ET = buf(BF16, (6, 512))
        u_mT_end = cur[0]
        attnT = buf(BF16, (8, T))
        rz = buf(F32, (2, 512))
        cuS = buf(F32, (2, 514))
        uS = buf(F32, (2, 514))
        cacc = buf(F32, (2, 512))
        ycT = buf(BF16, (8, T))
        gaS = buf(F32, (2, 512))
        gcS = buf(F32, (2, 512))
        yaS = buf(F32, (2, 512))
        u1 = cur[0]
        assert u_mT_end - u0 >= 16 * T * 2, (u_mT_end - u0)
        mT = view(u0, BF16, (16, T))
        c0 = u0 + 16 * T * 2
        qcT = view(c0, BF16, (4, T)); c0 += 4 * T * 2
        ETc = view(c0, BF16, (4, 512)); c0 += 4 * 512 * 2
        coinT = view(c0, BF16, (4, T)); c0 += 4 * T * 2
        rzc = view(c0, F32, (2, 512)); c0 += 2 * 512 * 4
        assert c0 <= u1
        f0 = u0
        aT = view(f0, BF16, (2, FGB, T)); f0 += 2 * FGB * T * 2
        sgS = view(f0, F32, (2, 512)); f0 += 2 * 512 * 4
        assert f0 <= u1, (f0, u1)
        xst = view(f0, F32, (4, D)); f0 += 4 * D * 4
        gF = view(f0, F32, (D,)); f0 += D * 4
        assert f0 <= u1, (f0, u1)
        memT = view(u0, BF16, (16, NMEM))
        print("SBUF bytes/partition used:", cur[0])

        bank_free = [True] * 8
        bank_rr = [0]

        def balloc():
            for i in range(8):
                b = (bank_rr[0] + i) % 8
                if bank_free[b]:
                    bank_free[b] = False
                    bank_rr[0] = (b + 1) % 8
                    return b
            raise RuntimeError("no free psum bank")

        def bfree(b):
            bank_free[b] = True

        def bank(b):
            return ps[:, b * 512:(b + 1) * 512]

        def mm(out, lhsT, rhs, start, stop):
            pr.add("pe", lambda e: e.matmul(out, lhsT, rhs, start=start, stop=stop), reads=[lhsT, rhs], writes=[out])

        def transpose(out, in_):
            pr.add("pe", lambda e: e.transpose(out, in_, ident), reads=[in_, ident], writes=[out])

        def act(out, in_, func, bias=None, scale=None, accum=None, extra_reads=()):
            kw = {}
            if bias is not None:
                kw["bias"] = bias
            if scale is not None:
                kw["scale"] = scale
            if accum is not None:
                kw["accum_out"] = accum
            rd = [in_] + [a for a in (bias, scale) if a is not None and not isinstance(a, float)] + list(extra_reads)
            wr = [out] + ([accum] if accum is not None else [])
            pr.add("act", lambda e: e.activation(out, in_, func, **kw), reads=rd, writes=wr)

        def tt(out, in0, in1, op):
            pr.add("dve", lambda e: e.tensor_tensor(out, in0, in1, op), reads=[in0, in1], writes=[out])

        def ts(out, in0, s1, s2, op0, op1):
            rd = [in0] + [a for a in (s1, s2) if a is not None and not isinstance(a, float)]
            if s2 is None:
                pr.add("dve", lambda e: e.tensor_scalar(out, in0, s1, None, op0), reads=rd, writes=[out])
            else:
                pr.add("dve", lambda e: e.tensor_scalar(out, in0, s1, s2, op0, op1), reads=rd, writes=[out])

        def stt(out, in0, scalar, in1, op0, op1):
            rd = [in0, in1] + ([scalar] if not isinstance(scalar, float) else [])
            pr.add("dve", lambda e: e.scalar_tensor_tensor(out, in0, scalar, in1, op0, op1), reads=rd, writes=[out])

        def recip(out, in_):
            pr.add("dve", lambda e: e.reciprocal(out, in_), reads=[in_], writes=[out])

        def vcopy(out, in_):
            pr.add("dve", lambda e: e.tensor_copy(out, in_), reads=[in_], writes=[out])

        def memset(out, val):
            pr.add("dve", lambda e: e.memset(out, val), reads=[], writes=[out])

        def dma(q, out, in_, sb_reads=(), sb_writes=()):
            return pr.add(q, lambda e: e.dma_start(out=out, in_=in_), reads=list(sb_reads), writes=list(sb_writes), is_dma=True)

        wstate = {"n": 0}

        def wload(piece_idx, nelem=2048):
            s = wstate["n"] % NBUF
            wstate["n"] += 1
            dst = wb[:, s, 0:nelem]
            dma("pool", dst, wp_d[piece_idx, :, 0:nelem], sb_writes=[dst])
            return wb[:, s, :]

        def wF(piece):
            return piece.rearrange("p (k c) -> p k c", c=128)

        def wT(piece, width=512):
            return piece.rearrange("p (k c) -> p k c", c=width)

        def bcast_last(ap2, n):
            l = [list(x) for x in ap2.ap] + [[0, n]]
            return bass.AP(ap2.tensor, ap2.offset, l)

        dbg_dma_ops = []
        dbg_stage = buf(F32, (2048,)) if dbg_d else None

        def dbg_dump(name, src, shape):
            n = int(np.prod(shape))
            flat = src if len(shape) == 1 else src.rearrange("p a b -> p (a b)")
            for o in range(0, n, 2048):
                w = min(2048, n - o)
                vcopy(dbg_stage[:, 0:w], flat[:, o:o + w])
                dbg_dma_ops.append(dma("sp", dbg_d[name][:, o:o + w], dbg_stage[:, 0:w], sb_reads=[dbg_stage[:, 0:w]]))

        dma("pool", cst, cst_d, sb_writes=[cst])
        dma("sp", small, small_d, sb_writes=[small])
        act(esink, esink, AF.Exp)
        vcopy(esB, bcast_last(esink, 128))

        out_dmas = []
        pending_final = []

        def norm_stats(srcs, ss_ready=False):
            nb = len(srcs)
            if not ss_ready:
                for i, src in enumerate(srcs):
                    act(xs[:, i % 2, :], src, AF.Square, accum=stat[:, i:i + 1])
            ts(stat[:, 8:8 + nb], stat[:, 0:nb], 1.0 / D, EPS, ALU.mult, ALU.add)
            act(stat[:, 16:16 + nb], stat[:, 8:8 + nb], AF.Ln)
            act(stat[:, 24:24 + nb], stat[:, 16:16 + nb], AF.Exp, scale=-0.5)

        def norm_apply_stt(srcs, i):
            for hf in range(2):
                cs_ = slice(hf * 1024, (hf + 1) * 1024)
                stt(xs[:, i % 2, cs_], srcs[i][:, cs_], stat[:, 24 + i:25 + i], gB[:, cs_], ALU.mult, ALU.mult)

        def norm_apply_T(dst, dcols, i):
            xsl = xs[:, i % 2, :]
            for half in range(2):
                b = balloc()
                bb = bank(b).bitcast(BF16)
                for cc in range(8):
                    c = half * 8 + cc
                    transpose(bb[:, cc * 128:(cc + 1) * 128], xsl[:, c * 128:(c + 1) * 128])
                o = dst[:, half * 8:half * 8 + 8, dcols[i]:dcols[i] + 128]
                act(o, bb[:, 0:1024].rearrange("p (a b) -> p a b", b=128), AF.Copy)
                bfree(b)

        def norm_apply(srcs, dst, dcols, idxs):
            for i in idxs:
                norm_apply_stt(srcs, i)
                norm_apply_T(dst, dcols, i)

        def load_g(gidx):
            dma("sp", gB, gv_d[gidx:gidx + 1, :].partition_broadcast(P), sb_writes=[gB])

        def norm_blocks(srcs, gidx, dst, dcols, ss_ready=False):
            if gidx is not None:
                load_g(gidx)
            norm_stats(srcs, ss_ready)
            norm_apply(srcs, dst, dcols, range(len(srcs)))

        def ss_from_parts():
            sv = ssp.rearrange("p (tb cg) -> p tb cg", cg=4)
            tt(stat[:, 0:4], sv[:, :, 0], sv[:, :, 1], ALU.add)
            tt(stat[:, 0:4], stat[:, 0:4], sv[:, :, 2], ALU.add)
            tt(stat[:, 0:4], stat[:, 0:4], sv[:, :, 3], ALU.add)

        def block_norm_stats(tb):
            sv = ssp[:, 4 * tb:4 * tb + 4]
            pr.add("dve", lambda e: e.reduce_sum(stat[:, tb:tb + 1], sv, mybir.AxisListType.X), reads=[sv], writes=[stat[:, tb:tb + 1]])
            ts(stat[:, 8 + tb:9 + tb], stat[:, tb:tb + 1], 1.0 / D, EPS, ALU.mult, ALU.add)
            act(stat[:, 16 + tb:17 + tb], stat[:, 8 + tb:9 + tb], AF.Ln)
            act(stat[:, 24 + tb:25 + tb], stat[:, 16 + tb:17 + tb], AF.Exp, scale=-0.5)

        def mixer_norm_loads(itn):
            tbn = 4 * itn
            srcs = []; dcols = []
            hsl = 0
            for e in range(6):
                tbk = tbn - 1 + e
                if tbk < 0 or tbk >= 16:
                    continue
                if 1 <= e <= 4:
                    dst = xst[:, e - 1, :]
                else:
                    dst = xh[:, hsl, :]; hsl += 1
                dma("sp", dst, x_d[tbk * 128:(tbk + 1) * 128, :], sb_writes=[dst])
                srcs.append(dst); dcols.append(e * 128)
            return srcs, dcols

        def mixer_norm_zero(itn):
            tbn = 4 * itn
            for e in range(6):
                tbk = tbn - 1 + e
                if tbk < 0 or tbk >= 16:
                    memset(hT[:, :, e * 128:(e + 1) * 128], 0.0)

        def load_cs(itn):
            dma("sp", cosT, cs_d[0, :, 512 * itn:512 * itn + EXT], sb_writes=[cosT])
            dma("sp", sinT, cs_d[1, :, 512 * itn:512 * itn + EXT], sb_writes=[sinT])

        def projF(piece, rhs_fn, N, KC=16, k0=0):
            b = balloc()
            w = wF(piece)
            for k in range(KC):
                mm(bank(b)[:, 0:N], w[:, k0 + k, :], rhs_fn(k), k == 0, k == KC - 1)
            return b

        rstate = {"n": 0}

        def rope(src_bank_ap, N, col0, out):
            sl = rstate["n"] % 2
            rstate["n"] += 1
            tA = rtmp[:, 2 * sl, 0:N]; tB = rtmp[:, 2 * sl + 1, 0:N]
            tt(tA, src_bank_ap, cosT[:, col0:col0 + N], ALU.mult)
            tt(tB[0:64, :], src_bank_ap[64:128, :], sinT[64:128, col0:col0 + N], ALU.mult)
            tt(tB[64:128, :], src_bank_ap[0:64, :], sinT[0:64, col0:col0 + N], ALU.mult)
            tt(out, tA, tB, ALU.add)

        load_cs(0)
        load_g(0)
        srcs0 = []; dcols0 = []
        for e in range(1, 5):
            dst = xst[:, e - 1, :]
            dma("sp", dst, x_d[(e - 1) * 128:e * 128, :], sb_writes=[dst])
            srcs0.append(dst); dcols0.append(e * 128)
        dma("sp", xh[:, 0, :], x_d[4 * 128:5 * 128, :], sb_writes=[xh[:, 0, :]])
        srcs0.append(xh[:, 0, :]); dcols0.append(5 * 128)
        mixer_norm_zero(0)
        norm_stats(srcs0)
        norm_apply(srcs0, hT, dcols0, range(len(srcs0)))

        def mem_stage():
            load_g(4)
            for j in range(2):
                dma("sp", xh[:, j, :], mem_d[j * 128:(j + 1) * 128, :], sb_writes=[xh[:, j, :]])
            norm_stats([xh[:, 0, :], xh[:, 1, :]])
            norm_apply([xh[:, 0, :], xh[:, 1, :]], memT, [0, 128], range(2))
            load_g(1)
            for h in range(4):
                pc = wload(h)
                b = projF(pc, lambda k: memT[:, k, :], NMEM)
                act(KmT[:, h, :], bank(b)[:, 0:NMEM], AF.Copy)
                bfree(b)
            bv = [balloc(), balloc()]
            for kg in range(4):
                pc = wT(wload(4 + kg))
                for kk in range(4):
                    k = 4 * kg + kk
                    for j in range(2):
                        mm(bank(bv[j]), memT[:, k, j * 128:(j + 1) * 128], pc[:, kk, :], k == 0, k == 15)
            for j in range(2):
                act(Vm[:, j, :], bank(bv[j]), AF.Copy)
                bfree(bv[j])

        def chk(tag):
            if stop == tag:
                raise _Stop()

        try:
          chk("mem")
          for it in range(ntiles):
              pi = [N_MEMP]

              def nextp(nelem=2048):
                  i = pi[0]
                  pi[0] += 1
                  return wload(i, nelem)

              tb0 = 4 * it

              def xview(cg):
                  return xh[:, cg % 2, :].rearrange("p (tb c) -> p tb c", c=512)

              def reload_issue(cg):
                  xv = xview(cg)
                  dma("sp", xv, x_d[tb0 * 128:(tb0 + 4) * 128, cg * 512:(cg + 1) * 512].rearrange("(tb p) c -> p tb c", p=128),
                      sb_writes=[xv])

              chk('norm')

              core = lambda k: hT[:, k, 128:640]
              for g in range(2):
                  pc = wF(nextp())
                  b2 = [balloc(), balloc()]
                  for k in range(16):
                      for hf in range(2):
                          mm(bank(b2[hf])[:, 0:384], pc[:, k, :], hT[:, k, hf * 384:(hf + 1) * 384], k == 0, k == 15)
                  for hf in range(2):
                      rope(bank(b2[hf])[:, 0:384], 384, hf * 384, kT[:, g, hf * 384:(hf + 1) * 384])
                      bfree(b2[hf])
              vp = [wT(nextp(), 256), wT(nextp(), 256)]
              for e in range(6):
                  b = balloc()
                  for k in range(16):
                      mm(bank(b)[:, 0:256], hT[:, k, e * 128:(e + 1) * 128], vp[k // 8][:, k % 8, :], k == 0, k == 15)
                  act(Vt[:, e, :], bank(b)[:, 0:256], AF.Copy)
                  bfree(b)

              if it == 0:
                  mem_stage()
              for h in range(8):
                  b = projF(nextp(), core, T)
                  rope(bank(b), T, 128, qT[:, h, :])
                  bfree(b)
                  if h % 2 == 1 and pending_final:
                      fb, tb_ = pending_final.pop(0)
                      fb(tb_)
              assert not pending_final
              chk('q')
              chk('qkv')
              if it + 1 < ntiles:
                  load_cs(it + 1)
              def attn_scores(g, n):
                  e = n + 1
                  info = []
                  for eb in (e - 1, e, e + 1):
                      tbk = tb0 - 1 + eb
                      if tbk < 0 or tbk >= 16:
                          continue
                      b = balloc()
                      msk = maskP if eb == e - 1 else (maskN if eb == e + 1 else None)
                      mm(bank(b).rearrange("p (a b) -> p a b", b=128), kT[:, g, eb * 128:(eb + 1) * 128],
                         qT[:, 4 * g:4 * g + 4, n * 128:(n + 1) * 128], True, msk is None)
                      if msk is not None:
                          mm(bank(b), ident, msk, False, True)
                      info.append((eb, b))
                  return info

              est = {"n": 0}

              def attn_exps(info):
                  slots = []
                  for (eb, b) in info:
                      sl = est["n"] % 6
                      est["n"] += 1
                      act(ET[:, sl, :], bank(b), AF.Exp, scale=SCALE)
                      bfree(b)
                      slots.append((eb, sl))
                  return slots

              def attn_finish(g, n, slots):
                  bo = balloc(); bz = balloc()
                  for i, (eb, sl) in enumerate(slots):
                      mm(bank(bo), Vt[:, eb, g * 128:(g + 1) * 128], ET[:, sl, :], i == 0, i == len(slots) - 1)
                  for i, (eb, sl) in enumerate(slots):
                      mm(bank(bz), ones, ET[:, sl, :], i == 0, i == len(slots) - 1)
                  r = rz[:, (g * 4 + n) % 2, :]
                  tt(r.rearrange("p (a b) -> p a b", b=128), bank(bz).rearrange("p (a b) -> p a b", b=128),
                     esB[:, 4 * g:4 * g + 4, :], ALU.add)
                  act(r, r, AF.Ln)
                  act(r, r, AF.Exp, scale=-1.0)
                  tt(attnT[:, 4 * g:4 * g + 4, n * 128:(n + 1) * 128], bank(bo).rearrange("p (a b) -> p a b", b=128),
                     r.rearrange("p (a b) -> p a b", b=128), ALU.mult)
                  bfree(bo); bfree(bz)

              win = [(127, 257), (384, 257)]
              for j in range(8):
                  g, n = j // 4, j % 4
                  info = attn_scores(g, n)
                  slots = attn_exps(info)
                  sl = j % 2
                  pc = wF(nextp())
                  bcu = [balloc(), balloc()]
                  for k in range(16):
                      for hf, (c0_, nn) in enumerate(win):
                          mm(bank(bcu[hf])[:, 0:nn], pc[:, k, :], hT[:, k, c0_:c0_ + nn], k == 0, k == 15)
                  for hf in range(2):
                      act(cuS[:, sl, hf * 257:(hf + 1) * 257], bank(bcu[hf])[:, 0:257], AF.Copy)
                      bfree(bcu[hf])
                  attn_finish(g, n, slots)
                  pc = wF(nextp())
                  bcc = [balloc(), balloc()]
                  for k in range(16):
                      for hf, (c0_, nn) in enumerate(win):
                          mm(bank(bcc[hf])[:, 0:nn], pc[:, k, :], hT[:, k, c0_:c0_ + nn], k == 0, k == 15)
                  for hf in range(2):
                      tt(uS[:, sl, hf * 257:(hf + 1) * 257], bank(bcc[hf])[:, 0:257], cuS[:, sl, hf * 257:(hf + 1) * 257], ALU.mult)
                      bfree(bcc[hf])
                  ca = cacc[:, sl, :]
                  act(ca, uS[:, sl, 0:512], AF.Copy, scale=convw[:, 3 * j:3 * j + 1])
                  stt(ca, uS[:, sl, 1:513], convw[:, 3 * j + 1:3 * j + 2], ca, ALU.mult, ALU.add)
                  stt(ca, uS[:, sl, 2:514], convw[:, 3 * j + 2:3 * j + 3], ca, ALU.mult, ALU.add)
                  b = projF(nextp(), core, T)
                  tt(ycT[:, j, :], bank(b), ca, ALU.mult)
                  bfree(b)
              if "attnT" in dbg_d and it == 0:
                  dbg_dump("attnT", attnT, (8, T)); dbg_dump("ycT", ycT, (8, T)); dbg_dump("qT", qT, (8, T))

              chk('attn')
              reload_issue(0); reload_issue(1)
              for c in range(16):
                  sl = c % 2
                  b1 = projF(nextp(), core, T)
                  act(gaS[:, sl, :], bank(b1), AF.Sigmoid, bias=bgate[:, c:c + 1])
                  bfree(b1)
                  b2_ = projF(nextp(), core, T)
                  act(gcS[:, sl, :], bank(b2_), AF.Sigmoid, bias=bgate[:, 16 + c:17 + c])
                  bfree(b2_)
                  pc = nextp()
                  b3 = projF(pc, lambda k: attnT[:, k, :], T, KC=8, k0=0)
                  b4 = projF(pc, lambda k: ycT[:, k, :], T, KC=8, k0=8)
                  tt(yaS[:, sl, :], bank(b3), gaS[:, sl, :], ALU.mult)
                  bfree(b3)
                  tt(gcS[:, sl, :], bank(b4), gcS[:, sl, :], ALU.mult)
                  bfree(b4)
                  tt(mT[:, c, :], gcS[:, sl, :], yaS[:, sl, :], ALU.add)

              def projT_res(src, nk_list, kc_total, reload_x=False, ssq=False, between=None, tail_norm=False):
                  for cg in range(4):
                      if tail_norm and cg == 3:
                          pieces = [wT(nextp(nk * 512)) for nk in nk_list]
                          xb = [xres[:, t_, :] for t_ in range(4)]
                          cols = [128 + 128 * t_ for t_ in range(4)]
                          for tb in range(4):
                              b = balloc()
                              k = 0
                              for pi_, nk in enumerate(nk_list):
                                  for kk in range(nk):
                                      mm(bank(b), src[:, k, tb * 128:(tb + 1) * 128], pieces[pi_][:, kk, :], k == 0, k == kc_total - 1)
                                      k += 1
                              xr = xres[:, tb, cg * 512:(cg + 1) * 512]
                              tt(xr, bank(b), xview(cg)[:, tb, :] if reload_x else xr, ALU.add)
                              bfree(b)
                              act(junk, xr, AF.Square, accum=ssp[:, tb * 4 + cg:tb * 4 + cg + 1])
                              block_norm_stats(tb)
                              norm_apply_stt(xb, tb)
                              if tb >= 1:
                                  norm_apply_T(hT, cols, tb - 1)
                          norm_apply_T(hT, cols, 3)
                          continue
                      bt = [balloc() for _ in range(4)]
                      k = 0
                      for nk in nk_list:
                          pc = wT(nextp(nk * 512))
                          for kk in range(nk):
                              for tb in range(4):
                                  mm(bank(bt[tb]), src[:, k, tb * 128:(tb + 1) * 128], pc[:, kk, :], k == 0, k == kc_total - 1)
                              k += 1
                      for tb in range(4):
                          xr = xres[:, tb, cg * 512:(cg + 1) * 512]
                          tt(xr, bank(bt[tb]), xview(cg)[:, tb, :] if reload_x else xr, ALU.add)
                          bfree(bt[tb])
                          if ssq:
                              act(junk, xr, AF.Square, accum=ssp[:, tb * 4 + cg:tb * 4 + cg + 1])
                      if reload_x and cg + 2 < 4:
                          reload_issue(cg + 2)
                      if between is not None:
                          between(cg)

              projT_res(mT, [4, 4, 4, 4], 16, reload_x=True, ssq=True, tail_norm=True)
              if "x1" in dbg_d and it == 0:
                  dbg_dump("x1", xres, (4, D))

              chk('mixer')
              load_g(2)
              def cq_proj(h):
                  b = projF(nextp(), core, T)
                  act(qcT[:, h, :], bank(b), AF.Copy)
                  bfree(b)

              def x_scores(hp):
                  bs = {}
                  for h in (2 * hp, 2 * hp + 1):
                      for j in range(2):
                          b = balloc()
                          mm(bank(b), KmT[:, h, j * 128:(j + 1) * 128], qcT[:, h, :], True, True)
                          bs[(h, j)] = b
                  return bs

              def x_exp(hp, bs):
                  for h in (2 * hp, 2 * hp + 1):
                      for j in range(2):
                          act(ETc[:, 2 * (h % 2) + j, :], bank(bs[(h, j)]), AF.Exp, scale=SCALE)
                          bfree(bs[(h, j)])

              def x_pv(hp):
                  boz = {}
                  for h in (2 * hp, 2 * hp + 1):
                      bo = balloc(); bz = balloc()
                      for j in range(2):
                          mm(bank(bo), Vm[:, j, h * 128:(h + 1) * 128], ETc[:, 2 * (h % 2) + j, :], j == 0, j == 1)
                      for j in range(2):
                          mm(bank(bz), ones, ETc[:, 2 * (h % 2) + j, :], j == 0, j == 1)
                      boz[h] = (bo, bz)
                  return boz

              def x_fin(hp, boz):
                  for h in (2 * hp, 2 * hp + 1):
                      bo, bz = boz[h]
                      r = rzc[:, h % 2, :]
                      act(r, bank(bz), AF.Ln)
                      act(r, r, AF.Exp, scale=-1.0)
                      tt(coinT[:, h, :], bank(bo), r, ALU.mult)
                      bfree(bo); bfree(bz)

              cq_proj(0); cq_proj(1); cq_proj(2)
              bs0 = x_scores(0)
              cq_proj(3)
              x_exp(0, bs0)
              bs1 = x_scores(1)
              boz0 = x_pv(0)
              x_exp(1, bs1)
              x_fin(0, boz0)
              boz1 = x_pv(1)
              x_fin(1, boz1)

              pcs = [wT(nextp(4 * 512)) for _ in range(4)]
              for tb in range(4):
                  bt = [balloc() for _ in range(4)]
                  for cg in range(4):
                      for k in range(4):
                          mm(bank(bt[cg]), coinT[:, k, tb * 128:(tb + 1) * 128], pcs[cg][:, k, :], k == 0, k == 3)
                  for cg in range(4):
                      xr = xres[:, tb, cg * 512:(cg + 1) * 512]
                      tt(xr, bank(bt[cg]), xr, ALU.add)
                      bfree(bt[cg])
                      act(junk, xr, AF.Square, accum=ssp[:, tb * 4 + cg:tb * 4 + cg + 1])
                  block_norm_stats(tb)
                  norm_apply_stt([xres[:, t_, :] for t_ in range(4)], tb)
                  if tb >= 1:
                      norm_apply_T(hT, [128 + 128 * t_ for t_ in range(4)], tb - 1)
              norm_apply_T(hT, [128 + 128 * t_ for t_ in range(4)], 3)
              if "x2" in dbg_d and it == 0:
                  dbg_dump("x2", xres, (4, D))

              chk('cross')
              dma("sp", gF, gv_d[3:4, :].partition_broadcast(P), sb_writes=[gF])
              if it + 1 < ntiles:
                  load_g(0)
                  nsrcs, ndcols = mixer_norm_loads(it + 1)
              for fg in range(FG):
                  asl = fg % 2
                  for jj in range(FGB):
                      sl = jj % 2
                      bg = projF(nextp(), core, T)
                      act(sgS[:, sl, :], bank(bg), AF.Silu)
                      bfree(bg)
                      bu = projF(nextp(), core, T)
                      tt(aT[:, asl, jj, :], bank(bu), sgS[:, sl, :], ALU.mult)
                      bfree(bu)
                  last = (fg == FG - 1)
                  if fg == 0 and it + 1 < ntiles:
                      norm_stats(nsrcs)
                  if last and it + 1 < ntiles:
                      nb_ = len(nsrcs)

                      def between(cg, nsrcs=nsrcs, ndcols=ndcols, nb_=nb_, itn=it + 1):
                          if cg == 0:
                              mixer_norm_zero(itn)
                              for i in range(min(2, nb_)):
                                  norm_apply_stt(nsrcs, i)
                          else:
                              for i in (2 * cg - 2, 2 * cg - 1):
                                  if i < nb_:
                                      norm_apply_T(hT, ndcols, i)
                                      if i + 2 < nb_:
                                          norm_apply_stt(nsrcs, i + 2)
                              if cg == 3:
                                  load_g(1)
                      projT_res(aT[:, asl], [4, 4, FGB - 8], FGB, ssq=True, between=between)
                  else:
                      projT_res(aT[:, asl], [4, 4, FGB - 8], FGB, ssq=last)
              assert pi[0] == N_MEMP + N_TILEP, pi[0]

              ss_from_parts()
              ts(stat[:, 8:12], stat[:, 0:4], 1.0 / D, EPS, ALU.mult, ALU.add)
              act(stat[:, 16:20], stat[:, 8:12], AF.Ln)
              act(stat[:, 24:28], stat[:, 16:20], AF.Exp, scale=-0.5)
              def final_block(tb, tb0=tb0):
                  o = xh[:, tb % 2, :]
                  stt(o, xres[:, tb, :], stat[:, 24 + tb:25 + tb], gF, ALU.mult, ALU.mult)
                  out_dmas.append(dma("sp", out_d[(tb0 + tb) * 128:(tb0 + tb + 1) * 128, :], o, sb_reads=[o]))

              if it + 1 < ntiles:
                  pending_final.extend([(final_block, tb) for tb in range(4)])
              else:
                  for tb in range(4):
                      final_block(tb)

        except _Stop:
            pass
        for d in dbg_dma_ops:
            out_dmas.append(d)
        pr.emit(es, out_dmas)
    return nc


def _F(W, m, KC):
    return np.ascontiguousarray(W[:, m * 128:(m + 1) * 128].reshape(KC, 128, 128).transpose(1, 0, 2)).reshape(128, KC * 128)


def _T(W, k0, nk, c0, width):
    return np.ascontiguousarray(W[k0 * 128:(k0 + nk) * 128, c0:c0 + width].reshape(nk, 128, width).transpose(1, 0, 2)).reshape(128, nk * width)


def prep_weights(w_in, w_attn_out, w_conv_out, w_o, w_cq, w_ckv, w_co, w_gate, w_up, w_down):
    Ws = {"w_in": w_in, "w_o": w_o, "w_cq": w_cq, "w_co": w_co, "w_gate": w_gate, "w_up": w_up, "w_down": w_down,
          "w_ck": w_ckv[:, :512], "w_cv": w_ckv[:, 512:]}
    order = mem_piece_order() + piece_order()
    wp = np.zeros((len(order), 128, 2048), np.float32)
    for i, pc in enumerate(order):
        if pc[0] == "F":
            W = Ws[pc[1]]
            a = _F(W, pc[2], W.shape[0] // 128)
        elif pc[0] == "Tv":
            a = _T(w_in, pc[1], 8, 1280, 256)
        elif pc[0] == "AOCO":
            c = pc[1]
            a = np.concatenate([_F(w_attn_out, c, 8), _F(w_conv_out, c, 8)], axis=1)
        else:
            _, name, k0, nk, cg = pc
            a = _T(Ws[name], k0, nk, cg * 512, 512)
        wp[i, :, :a.shape[1]] = a
    return wp


def prep_consts():
    inv = (1.0 / (np.float32(10000.0) ** (np.arange(0, 128, 2, dtype=np.float32) / np.float32(128)))).astype(np.float32)
    ang = np.arange(S, dtype=np.float32)[:, None] * inv[None, :]
    cos = np.cos(ang).astype(np.float32).T
    sin = np.sin(ang).astype(np.float32).T
    cs = np.zeros((2, 128, S + 256), np.float32)
    cs[0, :64, 128:128 + S] = cos; cs[0, 64:, 128:128 + S] = cos
    cs[1, :64, 128:128 + S] = sin; cs[1, 64:, 128:128 + S] = -sin
    cst = np.zeros((128, 1280), np.float32)
    cst[:, 0:128] = np.eye(128, dtype=np.float32)
    cst[:, 128:256] = 1.0
    j = np.arange(128)[:, None]; i = np.arange(128)[None, :]
    mp = np.where(j >= i, 0.0, -30000.0).astype(np.float32)
    mn = np.where(j <= i, 0.0, -30000.0).astype(np.float32)
    cst[:, 256:768] = np.tile(mp, (1, 4))
    cst[:, 768:1280] = np.tile(mn, (1, 4))
    return cs, cst


_CACHE = {}


def kernel(x, mem, g_mix, w_in, sink, conv_w, b_gate, w_attn_out, w_conv_out, w_o, g_cross, g_mem,
           w_cq, w_ckv, w_co, g_ffn, w_gate, w_up, w_down, g_final, _dbg=(), _ntiles=NT, _ncores=8, _stop=None):
    f = lambda a: np.asarray(a, dtype=np.float32)
    x = f(x); mem = f(mem)
    wp = prep_weights(f(w_in)[0], f(w_attn_out)[0], f(w_conv_out)[0], f(w_o)[0], f(w_cq)[0], f(w_ckv)[0], f(w_co)[0],
                      f(w_gate)[0], f(w_up)[0], f(w_down)[0])
    cs, cst = prep_consts()
    gv = np.stack([f(g_mix)[0], f(g_cross)[0], f(g_ffn)[0], f(g_final), f(g_mem)[0]], 0)
    small = np.zeros((128, 64), np.float32)
    cw = f(conv_w)[0]
    for j in range(8):
        for tap in range(3):
            small[:, 3 * j + tap] = cw[tap, j * 128:(j + 1) * 128]
    small[:, 24:56] = f(b_gate)[0].reshape(32, 128).T
    small[:, 56:64] = f(sink)[0][None, :]
    key = (tuple(_dbg), _ntiles, _stop)
    if key not in _CACHE:
        _CACHE[key] = build_program(_ntiles, _dbg, _stop)
    nc = _CACHE[key]
    in_maps = [{"x": np.ascontiguousarray(x[b]), "mem": np.ascontiguousarray(mem[b]), "wp": wp, "gv": gv,
                "small": small, "cs": cs, "cst": cst} for b in range(_ncores)]
    res = run_bass_kernel_spmd(nc, in_maps, core_ids=list(range(_ncores)))
    out = np.stack([r["out"] for r in res.results], 0).astype(np.float32)
    if _dbg:
        return out, res.results
    return out
```

```python
import numpy as np
from contextlib import ExitStack
import concourse.bass as bass
import concourse.mybir as mybir
from concourse.bass_utils import run_bass_kernel_spmd

F32 = mybir.dt.float32
BF16 = mybir.dt.bfloat16
ALU = mybir.AluOpType
AF = mybir.ActivationFunctionType

P = 128
D = 2048
S = 2048
T = 512
NT = S // T
EXT = T + 256
NMEM = 256
DFF = 5632
NFB = DFF // 128
FG = 4
FGB = NFB // FG
EPS = 1e-6
SCALE = 128 ** -0.5
NBUF = 6
DT_SIZE = {F32: 4, BF16: 2}


class Op:
    __slots__ = ("eng", "fn", "deps", "is_dma", "need_inc", "incval", "dsem", "dval", "idx")


def ap_interval(ap):
    es = DT_SIZE[ap.dtype]
    pstride = ap.ap[0][0]
    off = ap.offset % pstride if pstride > 0 else ap.offset
    ext = 1
    for st, cnt in ap.ap[1:]:
        ext += (cnt - 1) * abs(st)
    return (ap.tensor.name, off * es, (off + ext) * es)


class Prog:
    def __init__(self, nc):
        self.nc = nc
        self.ops = []
        self.spaces = {}
        self.engs = {"pe": nc.tensor, "act": nc.scalar, "dve": nc.vector, "sp": nc.sync, "pool": nc.gpsimd}

    def add(self, eng, fn, reads=(), writes=(), is_dma=False):
        op = Op()
        op.eng = eng; op.fn = fn; op.is_dma = is_dma; op.need_inc = False
        op.incval = 0; op.dsem = None; op.dval = 0; op.idx = len(self.ops)
        deps = {}
        for ap in reads:
            if ap is None or ap.tensor.name not in self.spaces and False:
                continue
            name, lo, hi = ap_interval(ap)
            sp = self.spaces.setdefault(name, {"w": [], "r": []})
            for (wl, wh, wop) in sp["w"]:
                if wl < hi and lo < wh:
                    deps[wop.idx] = (wop, True)
        for ap in writes:
            name, lo, hi = ap_interval(ap)
            sp = self.spaces.setdefault(name, {"w": [], "r": []})
            for (wl, wh, wop) in sp["w"]:
                if wl < hi and lo < wh and wop.idx not in deps:
                    deps[wop.idx] = (wop, False)
            for (rl, rh, rop) in sp["r"]:
                if rl < hi and lo < rh and rop.idx not in deps and rop is not op:
                    deps[rop.idx] = (rop, False)
        for ap in writes:
            name, lo, hi = ap_interval(ap)
            sp = self.spaces[name]
            sp["w"] = [e for e in sp["w"] if not (lo <= e[0] and e[1] <= hi)]
            sp["r"] = [e for e in sp["r"] if not (lo <= e[0] and e[1] <= hi)]
            sp["w"].append((lo, hi, op))
        for ap in reads:
            name, lo, hi = ap_interval(ap)
            sp = self.spaces[name]
            if not is_dma:
                sp["r"] = [e for e in sp["r"] if not (e[0] == lo and e[1] == hi and e[2].eng == eng and not e[2].is_dma)]
            sp["r"].append((lo, hi, op))
        final = []
        for (dop, raw) in deps.values():
            if dop.is_dma:
                final.append(dop)
            elif dop.eng == eng and not is_dma:
                if eng == "pe":
                    continue
                final.append(dop)
            else:
                final.append(dop)
        for dop in final:
            if not dop.is_dma:
                dop.need_inc = True
        op.deps = final
        self.ops.append(op)
        return op

    def emit(self, es, final_waits):
        nc = self.nc
        csem = {e: es.enter_context(nc.semaphore("c_" + e)) for e in ("pe", "act", "dve")}
        npool = {"sp": 24, "pool": 12}
        dsems = {q: [es.enter_context(nc.semaphore(f"d_{q}{i}")) for i in range(n)] for q, n in npool.items()}
        duse = {q: [0] * n for q, n in npool.items()}
        dcount = {q: 0 for q in npool}
        cnt = {e: 0 for e in csem}
        for op in self.ops:
            if op.is_dma:
                q = op.eng
                i = dcount[q] % npool[q]
                dcount[q] += 1
                duse[q][i] += 1
                op.dsem = (q, i)
                op.dval = 16 * duse[q][i]
            elif op.need_inc:
                cnt[op.eng] += 1
                op.incval = cnt[op.eng]
        waited = {e: {} for e in self.engs}
        for op in self.ops:
            E = self.engs[op.eng]
            w = waited[op.eng]
            need = {}
            if op.is_dma and op.dval > 16:
                key = ("d",) + op.dsem
                need[key] = max(need.get(key, 0), op.dval - 16)
            for d in op.deps:
                if d.is_dma:
                    key = ("d",) + d.dsem
                    need[key] = max(need.get(key, 0), d.dval)
                else:
                    key = ("c", d.eng)
                    need[key] = max(need.get(key, 0), d.incval)
            for key, val in need.items():
                if w.get(key, 0) >= val:
                    continue
                w[key] = val
                sem = csem[key[1]] if key[0] == "c" else dsems[key[1]][key[2]]
                E.wait_ge(sem, val)
            ins = op.fn(E)
            if op.is_dma:
                ins.then_inc(dsems[op.dsem[0]][op.dsem[1]], 16)
            elif op.need_inc:
                ins.then_inc(csem[op.eng], 1)
        E = nc.sync
        need = {}
        for d in final_waits:
            key = d.dsem
            need[key] = max(need.get(key, 0), d.dval)
        for key, val in need.items():
            E.wait_ge(dsems[key[0]][key[1]], val)


def piece_order():
    lst = []
    for m in (8, 9):
        lst.append(("F", "w_in", m))
    lst.append(("Tv", 0)); lst.append(("Tv", 8))
    for m in range(8):
        lst.append(("F", "w_in", m))
    for j in range(8):
        lst.append(("F", "w_in", 12 + j)); lst.append(("F", "w_in", 28 + j)); lst.append(("F", "w_in", 20 + j))
    for c in range(16):
        lst.append(("F", "w_in", 36 + c)); lst.append(("F", "w_in", 52 + c)); lst.append(("AOCO", c))
    for cg in range(4):
        for kg in range(4):
            lst.append(("T", "w_o", 4 * kg, 4, cg))
    for h in range(4):
        lst.append(("F", "w_cq", h))
    for cg in range(4):
        lst.append(("T", "w_co", 0, 4, cg))
    for fg in range(FG):
        for jj in range(FGB):
            lst.append(("F", "w_gate", fg * FGB + jj)); lst.append(("F", "w_up", fg * FGB + jj))
        for cg in range(4):
            for q in range(3):
                nk = 4 if q < 2 else FGB - 8
                lst.append(("T", "w_down", fg * FGB + 4 * q, nk, cg))
    return lst


def mem_piece_order():
    lst = []
    for h in range(4):
        lst.append(("F", "w_ck", h))
    for kg in range(4):
        lst.append(("T", "w_cv", 4 * kg, 4, 0))
    return lst


N_MEMP = 8
N_TILEP = len(piece_order())


class _Stop(Exception):
    pass


def build_program(ntiles=NT, dbg=(), stop=None):
    nc = bass.Bass("TRN2", target_bir_lowering=False)
    x_d = nc.dram_tensor("x", [S, D], F32, kind="ExternalInput").ap()
    mem_d = nc.dram_tensor("mem", [NMEM, D], F32, kind="ExternalInput").ap()
    wp_d = nc.dram_tensor("wp", [N_MEMP + N_TILEP, P, 2048], F32, kind="ExternalInput").ap()
    gv_d = nc.dram_tensor("gv", [5, D], F32, kind="ExternalInput").ap()
    small_d = nc.dram_tensor("small", [P, 64], F32, kind="ExternalInput").ap()
    cs_d = nc.dram_tensor("cs", [2, P, S + 256], F32, kind="ExternalInput").ap()
    cst_d = nc.dram_tensor("cst", [P, 1280], F32, kind="ExternalInput").ap()
    out_d = nc.dram_tensor("out", [S, D], F32, kind="ExternalOutput").ap()
    dbg_d = {}
    for name, shape in dbg:
        dbg_d[name] = nc.dram_tensor("dbg_" + name, list(shape), F32, kind="ExternalOutput").ap()

    with ExitStack() as es:
        ARENA = 207 * 1024
        arena = es.enter_context(nc.sbuf_tensor("arena", [P, ARENA // 2], BF16))
        ps = es.enter_context(nc.psum_tensor("ps", [P, 4096], F32))
        pr = Prog(nc)
        cur = [0]

        def alloc(nbytes):
            lo = cur[0]
            cur[0] += (nbytes + 63) // 64 * 64
            assert cur[0] <= ARENA, cur[0]
            return lo

        def view(lo, dt, shape):
            n = int(np.prod(shape))
            es_ = DT_SIZE[dt]
            a = arena[:, lo // 2: lo // 2 + n * es_ // 2]
            if dt != BF16:
                a = a.bitcast(dt)
            if len(shape) == 2:
                a = a.rearrange("p (a b) -> p a b", b=shape[1])
            elif len(shape) == 3:
                a = a.rearrange("p (a b c) -> p a b c", b=shape[1], c=shape[2])
            return a

        def buf(dt, shape):
            return view(alloc(int(np.prod(shape)) * DT_SIZE[dt]), dt, shape)

        xres = buf(F32, (4, D))
        hT = buf(BF16, (16, EXT))
        xs = buf(BF16, (2, D))
        gB = buf(F32, (D,))
        xh = buf(F32, (2, D))
        cosT = buf(F32, (EXT,))
        sinT = buf(F32, (EXT,))
        wb = buf(BF16, (NBUF, 2048))
        cst = buf(BF16, (1280,))
        ident = cst[:, 0:128]
        ones = cst[:, 128:256]
        maskP = cst[:, 256:768]
        maskN = cst[:, 768:1280]
        small = buf(F32, (64,))
        convw = small[:, 0:24]
        bgate = small[:, 24:56]
        esink = small[:, 56:64]
        esB = buf(F32, (8, 128))
        stat = buf(F32, (32,))
        ssp = buf(F32, (16,))
        epsb = buf(F32, (1,))
        junk = buf(BF16, (512,))
        KmT = buf(BF16, (4, NMEM))
        Vm = buf(BF16, (2, 512))
        u0 = cur[0]
        qT = buf(BF16, (8, T))
        kT = buf(BF16, (2, EXT))
        Vt = buf(BF16, (6, 256))
        rtmp = buf(F32, (4, 512))
        ET = buf(BF16, (6, 512))
        u_mT_end = cur[0]
        attnT = buf(BF16, (8, T))
        rz = buf(F32, (2, 512))
        cuS = buf(F32, (2, 514))
        uS = buf(F32, (2, 514))
        cacc = buf(F32, (2, 512))
        ycT = buf(BF16, (8, T))
        gaS = buf(F32, (2, 512))
        gcS = buf(F32, (2, 512))
        yaS = buf(F32, (2, 512))
        u1 = cur[0]
        assert u_mT_end - u0 >= 16 * T * 2, (u_mT_end - u0)
        mT = view(u0, BF16, (16, T))
        c0 = u0 + 16 * T * 2
        qcT = view(c0, BF16, (4, T)); c0 += 4 * T * 2
        ETc = view(c0, BF16, (4, 512)); c0 += 4 * 512 * 2
        coinT = view(c0, BF16, (4, T)); c0 += 4 * T * 2
        rzc = view(c0, F32, (2, 512)); c0 += 2 * 512 * 4
        assert c0 <= u1
        f0 = u0
        aT = view(f0, BF16, (2, FGB, T)); f0 += 2 * FGB * T * 2
        sgS = view(f0, F32, (2, 512)); f0 += 2 * 512 * 4
        assert f0 <= u1, (f0, u1)
        xst = view(f0, F32, (4, D)); f0 += 4 * D * 4
        gF = view(f0, F32, (D,)); f0 += D * 4
        assert f0 <= u1, (f0, u1)
        memT = view(u0, BF16, (16, NMEM))
        print("SBUF bytes/partition used:", cur[0])

        bank_free = [True] * 8
        bank_rr = [0]

        def balloc():
            for i in range(8):
                b = (bank_rr[0] + i) % 8
                if bank_free[b]:
                    bank_free[b] = False
                    bank_rr[0] = (b + 1) % 8
                    return b
            raise RuntimeError("no free psum bank")

        def bfree(b):
            bank_free[b] = True

        def bank(b):
            return ps[:, b * 512:(b + 1) * 512]

        def mm(out, lhsT, rhs, start, stop):
            pr.add("pe", lambda e: e.matmul(out, lhsT, rhs, start=start, stop=stop), reads=[lhsT, rhs], writes=[out])

        def transpose(out, in_):
            pr.add("pe", lambda e: e.transpose(out, in_, ident), reads=[in_, ident], writes=[out])

        def act(out, in_, func, bias=None, scale=None, accum=None, extra_reads=()):
            kw = {}
            if bias is not None:
                kw["bias"] = bias
            if scale is not None:
                kw["scale"] = scale
            if accum is not None:
                kw["accum_out"] = accum
            rd = [in_] + [a for a in (bias, scale) if a is not None and not isinstance(a, float)] + list(extra_reads)
            wr = [out] + ([accum] if accum is not None else [])
            pr.add("act", lambda e: e.activation(out, in_, func, **kw), reads=rd, writes=wr)

        def tt(out, in0, in1, op):
            pr.add("dve", lambda e: e.tensor_tensor(out, in0, in1, op), reads=[in0, in1], writes=[out])

        def ts(out, in0, s1, s2, op0, op1):
            rd = [in0] + [a for a in (s1, s2) if a is not None and not isinstance(a, float)]
            if s2 is None:
                pr.add("dve", lambda e: e.tensor_scalar(out, in0, s1, None, op0), reads=rd, writes=[out])
            else:
                pr.add("dve", lambda e: e.tensor_scalar(out, in0, s1, s2, op0, op1), reads=rd, writes=[out])

        def stt(out, in0, scalar, in1, op0, op1):
            rd = [in0, in1] + ([scalar] if not isinstance(scalar, float) else [])
            pr.add("dve", lambda e: e.scalar_tensor_tensor(out, in0, scalar, in1, op0, op1), reads=rd, writes=[out])

        def recip(out, in_):
            pr.add("dve", lambda e: e.reciprocal(out, in_), reads=[in_], writes=[out])

        def vcopy(out, in_):
            pr.add("dve", lambda e: e.tensor_copy(out, in_), reads=[in_], writes=[out])

        def memset(out, val):
            pr.add("dve", lambda e: e.memset(out, val), reads=[], writes=[out])

        def dma(q, out, in_, sb_reads=(), sb_writes=()):
            return pr.add(q, lambda e: e.dma_start(out=out, in_=in_), reads=list(sb_reads), writes=list(sb_writes), is_dma=True)

        wstate = {"n": 0}

        def wload(piece_idx, nelem=2048):
            s = wstate["n"] % NBUF
            wstate["n"] += 1
            dst = wb[:, s, 0:nelem]
            dma("pool", dst, wp_d[piece_idx, :, 0:nelem], sb_writes=[dst])
            return wb[:, s, :]

        def wF(piece):
            return piece.rearrange("p (k c) -> p k c", c=128)

        def wT(piece, width=512):
            return piece.rearrange("p (k c) -> p k c", c=width)

        def bcast_last(ap2, n):
            l = [list(x) for x in ap2.ap] + [[0, n]]
            return bass.AP(ap2.tensor, ap2.offset, l)

        dbg_dma_ops = []
        dbg_stage = buf(F32, (2048,)) if dbg_d else None

        def dbg_dump(name, src, shape):
            n = int(np.prod(shape))
            flat = src if len(shape) == 1 else src.rearrange("p a b -> p (a b)")
            for o in range(0, n, 2048):
                w = min(2048, n - o)
                vcopy(dbg_stage[:, 0:w], flat[:, o:o + w])
                dbg_dma_ops.append(dma("sp", dbg_d[name][:, o:o + w], dbg_stage[:, 0:w], sb_reads=[dbg_stage[:, 0:w]]))

        memset(epsb, EPS)
        dma("pool", cst, cst_d, sb_writes=[cst])
        dma("sp", small, small_d, sb_writes=[small])
        act(esink, esink, AF.Exp)
        vcopy(esB, bcast_last(esink, 128))

        out_dmas = []
        pending_final = []

        def norm_stats(srcs, ss_ready=False):
            nb = len(srcs)
            if not ss_ready:
                for i, src in enumerate(srcs):
                    act(xs[:, i % 2, :], src, AF.Square, accum=stat[:, i:i + 1])
            act(stat[:, 16:16 + nb], stat[:, 0:nb], AF.Ln, bias=epsb, scale=1.0 / D)
            act(stat[:, 24:24 + nb], stat[:, 16:16 + nb], AF.Exp, scale=-0.5)

        def norm_apply_stt(srcs, i):
            for hf in range(2):
                cs_ = slice(hf * 1024, (hf + 1) * 1024)
                stt(xs[:, i % 2, cs_], srcs[i][:, cs_], stat[:, 24 + i:25 + i], gB[:, cs_], ALU.mult, ALU.mult)

        def norm_apply_T(dst, dcols, i):
            xsl = xs[:, i % 2, :]
            for half in range(2):
                b = balloc()
                bb = bank(b).bitcast(BF16)
                for cc in range(8):
                    c = half * 8 + cc
                    transpose(bb[:, cc * 128:(cc + 1) * 128], xsl[:, c * 128:(c + 1) * 128])
                o = dst[:, half * 8:half * 8 + 8, dcols[i]:dcols[i] + 128]
                act(o, bb[:, 0:1024].rearrange("p (a b) -> p a b", b=128), AF.Copy)
                bfree(b)

        def norm_apply(srcs, dst, dcols, idxs):
            for i in idxs:
                norm_apply_stt(srcs, i)
                norm_apply_T(dst, dcols, i)

        def load_g(gidx):
            dma("sp", gB, gv_d[gidx:gidx + 1, :].partition_broadcast(P), sb_writes=[gB])

        def norm_blocks(srcs, gidx, dst, dcols, ss_ready=False):
            if gidx is not None:
                load_g(gidx)
            norm_stats(srcs, ss_ready)
            norm_apply(srcs, dst, dcols, range(len(srcs)))

        def ss_from_parts():
            sv = ssp.rearrange("p (tb cg) -> p tb cg", cg=4)
            tt(stat[:, 0:4], sv[:, :, 0], sv[:, :, 1], ALU.add)
            tt(stat[:, 0:4], stat[:, 0:4], sv[:, :, 2], ALU.add)
            tt(stat[:, 0:4], stat[:, 0:4], sv[:, :, 3], ALU.add)

        def block_norm_stats(tb):
            sv = ssp[:, 4 * tb:4 * tb + 4]
            pr.add("dve", lambda e: e.reduce_sum(stat[:, tb:tb + 1], sv, mybir.AxisListType.X), reads=[sv], writes=[stat[:, tb:tb + 1]])
            act(stat[:, 16 + tb:17 + tb], stat[:, tb:tb + 1], AF.Ln, bias=epsb, scale=1.0 / D)
            act(stat[:, 24 + tb:25 + tb], stat[:, 16 + tb:17 + tb], AF.Exp, scale=-0.5)

        def mixer_norm_loads(itn):
            tbn = 4 * itn
            srcs = []; dcols = []
            hsl = 0
            for e in range(6):
                tbk = tbn - 1 + e
                if tbk < 0 or tbk >= 16:
                    continue
                if 1 <= e <= 4:
                    dst = xst[:, e - 1, :]
                else:
                    dst = xh[:, hsl, :]; hsl += 1
                dma("sp", dst, x_d[tbk * 128:(tbk + 1) * 128, :], sb_writes=[dst])
                srcs.append(dst); dcols.append(e * 128)
            return srcs, dcols

        def mixer_norm_zero(itn):
            tbn = 4 * itn
            for e in range(6):
                tbk = tbn - 1 + e
                if tbk < 0 or tbk >= 16:
                    memset(hT[:, :, e * 128:(e + 1) * 128], 0.0)

        def load_cs(itn):
            dma("sp", cosT, cs_d[0, :, 512 * itn:512 * itn + EXT], sb_writes=[cosT])
            dma("sp", sinT, cs_d[1, :, 512 * itn:512 * itn + EXT], sb_writes=[sinT])

        def projF(piece, rhs_fn, N, KC=16, k0=0):
            b = balloc()
            w = wF(piece)
            for k in range(KC):
                mm(bank(b)[:, 0:N], w[:, k0 + k, :], rhs_fn(k), k == 0, k == KC - 1)
            return b

        rstate = {"n": 0}

        def rope(src_bank_ap, N, col0, out):
            sl = rstate["n"] % 2
            rstate["n"] += 1
            tA = rtmp[:, 2 * sl, 0:N]; tB = rtmp[:, 2 * sl + 1, 0:N]
            tt(tA, src_bank_ap, cosT[:, col0:col0 + N], ALU.mult)
            tt(tB[0:64, :], src_bank_ap[64:128, :], sinT[64:128, col0:col0 + N], ALU.mult)
            tt(tB[64:128, :], src_bank_ap[0:64, :], sinT[0:64, col0:col0 + N], ALU.mult)
            tt(out, tA, tB, ALU.add)

        load_cs(0)
        load_g(0)
        srcs0 = []; dcols0 = []
        for e in range(1, 5):
            dst = xst[:, e - 1, :]
            dma("sp", dst, x_d[(e - 1) * 128:e * 128, :], sb_writes=[dst])
            srcs0.append(dst); dcols0.append(e * 128)
        dma("sp", xh[:, 0, :], x_d[4 * 128:5 * 128, :], sb_writes=[xh[:, 0, :]])
        srcs0.append(xh[:, 0, :]); dcols0.append(5 * 128)
        mixer_norm_zero(0)
        for i, src in enumerate(srcs0):
            act(xs[:, i % 2, :], src, AF.Square, accum=stat[:, i:i + 1])
            act(stat[:, 16 + i:17 + i], stat[:, i:i + 1], AF.Ln, bias=epsb, scale=1.0 / D)
            act(stat[:, 24 + i:25 + i], stat[:, 16 + i:17 + i], AF.Exp, scale=-0.5)
            norm_apply_stt(srcs0, i)
            norm_apply_T(hT, dcols0, i)

        def mem_stage():
            load_g(4)
            for j in range(2):
                dma("sp", xh[:, j, :], mem_d[j * 128:(j + 1) * 128, :], sb_writes=[xh[:, j, :]])
            norm_stats([xh[:, 0, :], xh[:, 1, :]])
            norm_apply([xh[:, 0, :], xh[:, 1, :]], memT, [0, 128], range(2))
            load_g(1)
            for h in range(4):
                pc = wload(h)
                b = projF(pc, lambda k: memT[:, k, :], NMEM)
                act(KmT[:, h, :], bank(b)[:, 0:NMEM], AF.Copy)
                bfree(b)
            bv = [balloc(), balloc()]
            for kg in range(4):
                pc = wT(wload(4 + kg))
                for kk in range(4):
                    k = 4 * kg + kk
                    for j in range(2):
                        mm(bank(bv[j]), memT[:, k, j * 128:(j + 1) * 128], pc[:, kk, :], k == 0, k == 15)
            for j in range(2):
                act(Vm[:, j, :], bank(bv[j]), AF.Copy)
                bfree(bv[j])

        def chk(tag):
            if stop == tag:
                raise _Stop()

        try:
          chk("mem")
          for it in range(ntiles):
              pi = [N_MEMP]

              def nextp(nelem=2048):
                  i = pi[0]
                  pi[0] += 1
                  return wload(i, nelem)

              tb0 = 4 * it

              def xview(cg):
                  return xh[:, cg % 2, :].rearrange("p (tb c) -> p tb c", c=512)

              def reload_issue(cg):
                  xv = xview(cg)
                  dma("sp", xv, x_d[tb0 * 128:(tb0 + 4) * 128, cg * 512:(cg + 1) * 512].rearrange("(tb p) c -> p tb c", p=128),
                      sb_writes=[xv])

              chk('norm')

              core = lambda k: hT[:, k, 128:640]
              for g in range(2):
                  pc = wF(nextp())
                  b2 = [balloc(), balloc()]
                  for k in range(16):
                      for hf in range(2):
                          mm(bank(b2[hf])[:, 0:384], pc[:, k, :], hT[:, k, hf * 384:(hf + 1) * 384], k == 0, k == 15)
                  for hf in range(2):
                      rope(bank(b2[hf])[:, 0:384], 384, hf * 384, kT[:, g, hf * 384:(hf + 1) * 384])
                      bfree(b2[hf])
              vp = [wT(nextp(), 256), wT(nextp(), 256)]
              for e in range(6):
                  b = balloc()
                  for k in range(16):
                      mm(bank(b)[:, 0:256], hT[:, k, e * 128:(e + 1) * 128], vp[k // 8][:, k % 8, :], k == 0, k == 15)
                  act(Vt[:, e, :], bank(b)[:, 0:256], AF.Copy)
                  bfree(b)

              if it == 0:
                  mem_stage()
              for h in range(8):
                  b = projF(nextp(), core, T)
                  rope(bank(b), T, 128, qT[:, h, :])
                  bfree(b)
                  if h % 2 == 1 and pending_final:
                      fb, tb_ = pending_final.pop(0)
                      fb(tb_)
              assert not pending_final
              chk('q')
              chk('qkv')
              if it + 1 < ntiles:
                  load_cs(it + 1)
              def attn_scores(g, n):
                  e = n + 1
                  info = []
                  for eb in (e - 1, e, e + 1):
                      tbk = tb0 - 1 + eb
                      if tbk < 0 or tbk >= 16:
                          continue
                      b = balloc()
                      msk = maskP if eb == e - 1 else (maskN if eb == e + 1 else None)
                      mm(bank(b).rearrange("p (a b) -> p a b", b=128), kT[:, g, eb * 128:(eb + 1) * 128],
                         qT[:, 4 * g:4 * g + 4, n * 128:(n + 1) * 128], True, msk is None)
                      if msk is not None:
                          mm(bank(b), ident, msk, False, True)
                      info.append((eb, b))
                  return info

              est = {"n": 0}

              def attn_exps(info):
                  slots = []
                  for (eb, b) in info:
                      sl = est["n"] % 6
                      est["n"] += 1
                      act(ET[:, sl, :], bank(b), AF.Exp, scale=SCALE)
                      bfree(b)
                      slots.append((eb, sl))
                  return slots

              def attn_finish(g, n, slots):
                  bo = balloc(); bz = balloc()
                  for i, (eb, sl) in enumerate(slots):
                      mm(bank(bo), Vt[:, eb, g * 128:(g + 1) * 128], ET[:, sl, :], i == 0, i == len(slots) - 1)
                  for i, (eb, sl) in enumerate(slots):
                      mm(bank(bz), ones, ET[:, sl, :], i == 0, i == len(slots) - 1)
                  r = rz[:, (g * 4 + n) % 2, :]
                  tt(r.rearrange("p (a b) -> p a b", b=128), bank(bz).rearrange("p (a b) -> p a b", b=128),
                     esB[:, 4 * g:4 * g + 4, :], ALU.add)
                  act(r, r, AF.Ln)
                  act(r, r, AF.Exp, scale=-1.0)
                  tt(attnT[:, 4 * g:4 * g + 4, n * 128:(n + 1) * 128], bank(bo).rearrange("p (a b) -> p a b", b=128),
                     r.rearrange("p (a b) -> p a b", b=128), ALU.mult)
                  bfree(bo); bfree(bz)

              win = [(127, 257), (384, 257)]
              for j in range(8):
                  g, n = j // 4, j % 4
                  info = attn_scores(g, n)
                  slots = attn_exps(info)
                  sl = j % 2
                  pc = wF(nextp())
                  bcu = [balloc(), balloc()]
                  for k in range(16):
                      for hf, (c0_, nn) in enumerate(win):
                          mm(bank(bcu[hf])[:, 0:nn], pc[:, k, :], hT[:, k, c0_:c0_ + nn], k == 0, k == 15)
                  for hf in range(2):
                      act(cuS[:, sl, hf * 257:(hf + 1) * 257], bank(bcu[hf])[:, 0:257], AF.Copy)
                      bfree(bcu[hf])
                  attn_finish(g, n, slots)
                  pc = wF(nextp())
                  bcc = [balloc(), balloc()]
                  for k in range(16):
                      for hf, (c0_, nn) in enumerate(win):
                          mm(bank(bcc[hf])[:, 0:nn], pc[:, k, :], hT[:, k, c0_:c0_ + nn], k == 0, k == 15)
                  for hf in range(2):
                      tt(uS[:, sl, hf * 257:(hf + 1) * 257], bank(bcc[hf])[:, 0:257], cuS[:, sl, hf * 257:(hf + 1) * 257], ALU.mult)
                      bfree(bcc[hf])
                  ca = cacc[:, sl, :]
                  act(ca, uS[:, sl, 0:512], AF.Copy, scale=convw[:, 3 * j:3 * j + 1])
                  stt(ca, uS[:, sl, 1:513], convw[:, 3 * j + 1:3 * j + 2], ca, ALU.mult, ALU.add)
                  stt(ca, uS[:, sl, 2:514], convw[:, 3 * j + 2:3 * j + 3], ca, ALU.mult, ALU.add)
                  b = projF(nextp(), core, T)
                  tt(ycT[:, j, :], bank(b), ca, ALU.mult)
                  bfree(b)
              if "attnT" in dbg_d and it == 0:
                  dbg_dump("attnT", attnT, (8, T)); dbg_dump("ycT", ycT, (8, T)); dbg_dump("qT", qT, (8, T))

              chk('attn')
              reload_issue(0); reload_issue(1)
              for c in range(16):
                  sl = c % 2
                  b1 = projF(nextp(), core, T)
                  act(gaS[:, sl, :], bank(b1), AF.Sigmoid, bias=bgate[:, c:c + 1])
                  bfree(b1)
                  b2_ = projF(nextp(), core, T)
                  act(gcS[:, sl, :], bank(b2_), AF.Sigmoid, bias=bgate[:, 16 + c:17 + c])
                  bfree(b2_)
                  pc = nextp()
                  b3 = projF(pc, lambda k: attnT[:, k, :], T, KC=8, k0=0)
                  b4 = projF(pc, lambda k: ycT[:, k, :], T, KC=8, k0=8)
                  tt(yaS[:, sl, :], bank(b3), gaS[:, sl, :], ALU.mult)
                  bfree(b3)
                  tt(gcS[:, sl, :], bank(b4), gcS[:, sl, :], ALU.mult)
                  bfree(b4)
                  tt(mT[:, c, :], gcS[:, sl, :], yaS[:, sl, :], ALU.add)

              def projT_res(src, nk_list, kc_total, reload_x=False, ssq=False, between=None, tail_norm=False):
                  for cg in range(4):
                      if tail_norm and cg == 3:
                          pieces = [wT(nextp(nk * 512)) for nk in nk_list]
                          xb = [xres[:, t_, :] for t_ in range(4)]
                          cols = [128 + 128 * t_ for t_ in range(4)]
                          for tb in range(4):
                              b = balloc()
                              k = 0
                              for pi_, nk in enumerate(nk_list):
                                  for kk in range(nk):
                                      mm(bank(b), src[:, k, tb * 128:(tb + 1) * 128], pieces[pi_][:, kk, :], k == 0, k == kc_total - 1)
                                      k += 1
                              xr = xres[:, tb, cg * 512:(cg + 1) * 512]
                              tt(xr, bank(b), xview(cg)[:, tb, :] if reload_x else xr, ALU.add)
                              bfree(b)
                              act(junk, xr, AF.Square, accum=ssp[:, tb * 4 + cg:tb * 4 + cg + 1])
                              block_norm_stats(tb)
                              norm_apply_stt(xb, tb)
                              if tb >= 1:
                                  norm_apply_T(hT, cols, tb - 1)
                          norm_apply_T(hT, cols, 3)
                          continue
                      bt = [balloc() for _ in range(4)]
                      k = 0
                      for nk in nk_list:
                          pc = wT(nextp(nk * 512))
                          for kk in range(nk):
                              for tb in range(4):
                                  mm(bank(bt[tb]), src[:, k, tb * 128:(tb + 1) * 128], pc[:, kk, :], k == 0, k == kc_total - 1)
                              k += 1
                      for tb in range(4):
                          xr = xres[:, tb, cg * 512:(cg + 1) * 512]
                          tt(xr, bank(bt[tb]), xview(cg)[:, tb, :] if reload_x else xr, ALU.add)
                          bfree(bt[tb])
                          if ssq:
                              act(junk, xr, AF.Square, accum=ssp[:, tb * 4 + cg:tb * 4 + cg + 1])
                      if reload_x and cg + 2 < 4:
                          reload_issue(cg + 2)
                      if between is not None:
                          between(cg)

              projT_res(mT, [4, 4, 4, 4], 16, reload_x=True, ssq=True, tail_norm=True)
              if "x1" in dbg_d and it == 0:
                  dbg_dump("x1", xres, (4, D))

              chk('mixer')
              load_g(2)
              def cq_proj(h):
                  b = projF(nextp(), core, T)
                  act(qcT[:, h, :], bank(b), AF.Copy)
                  bfree(b)

              def x_scores(hp):
                  bs = {}
                  for h in (2 * hp, 2 * hp + 1):
                      for j in range(2):
                          b = balloc()
                          mm(bank(b), KmT[:, h, j * 128:(j + 1) * 128], qcT[:, h, :], True, True)
                          bs[(h, j)] = b
                  return bs

              def x_exp(hp, bs):
                  for h in (2 * hp, 2 * hp + 1):
                      for j in range(2):
                          act(ETc[:, 2 * (h % 2) + j, :], bank(bs[(h, j)]), AF.Exp, scale=SCALE)
                          bfree(bs[(h, j)])

              def x_pv(hp):
                  boz = {}
                  for h in (2 * hp, 2 * hp + 1):
                      bo = balloc(); bz = balloc()
                      for j in range(2):
                          mm(bank(bo), Vm[:, j, h * 128:(h + 1) * 128], ETc[:, 2 * (h % 2) + j, :], j == 0, j == 1)
                      for j in range(2):
                          mm(bank(bz), ones, ETc[:, 2 * (h % 2) + j, :], j == 0, j == 1)
                      boz[h] = (bo, bz)
                  return boz

              def x_fin(hp, boz):
                  for h in (2 * hp, 2 * hp + 1):
                      bo, bz = boz[h]
                      r = rzc[:, h % 2, :]
                      act(r, bank(bz), AF.Ln)
                      act(r, r, AF.Exp, scale=-1.0)
                      tt(coinT[:, h, :], bank(bo), r, ALU.mult)
                      bfree(bo); bfree(bz)

              cq_proj(0); cq_proj(1); cq_proj(2)
              bs0 = x_scores(0)
              cq_proj(3)
              x_exp(0, bs0)
              bs1 = x_scores(1)
              boz0 = x_pv(0)
              x_exp(1, bs1)
              x_fin(0, boz0)
              boz1 = x_pv(1)
              x_fin(1, boz1)

              pcs = [wT(nextp(4 * 512)) for _ in range(4)]
              for tb in range(4):
                  bt = [balloc() for _ in range(4)]
                  for cg in range(4):
                      for k in range(4):
                          mm(bank(bt[cg]), coinT[:, k, tb * 128:(tb + 1) * 128], pcs[cg][:, k, :], k == 0, k == 3)
                  for cg in range(4):
                      xr = xres[:, tb, cg * 512:(cg + 1) * 512]
                      tt(xr, bank(bt[cg]), xr, ALU.add)
                      bfree(bt[cg])
                      act(junk, xr, AF.Square, accum=ssp[:, tb * 4 + cg:tb * 4 + cg + 1])
                  block_norm_stats(tb)
                  norm_apply_stt([xres[:, t_, :] for t_ in range(4)], tb)
                  if tb >= 1:
                      norm_apply_T(hT, [128 + 128 * t_ for t_ in range(4)], tb - 1)
              norm_apply_T(hT, [128 + 128 * t_ for t_ in range(4)], 3)
              if "x2" in dbg_d and it == 0:
                  dbg_dump("x2", xres, (4, D))

              chk('cross')
              dma("sp", gF, gv_d[3:4, :].partition_broadcast(P), sb_writes=[gF])
              if it + 1 < ntiles:
                  load_g(0)
                  nsrcs, ndcols = mixer_norm_loads(it + 1)
              for fg in range(FG):
                  asl = fg % 2
                  for jj in range(FGB):
                      sl = jj % 2
                      bg = projF(nextp(), core, T)
                      act(sgS[:, sl, :], bank(bg), AF.Silu)
                      bfree(bg)
                      bu = projF(nextp(), core, T)
                      tt(aT[:, asl, jj, :], bank(bu), sgS[:, sl, :], ALU.mult)
                      bfree(bu)
                  last = (fg == FG - 1)
                  if fg == 0 and it + 1 < ntiles:
                      norm_stats(nsrcs)
                  if last and it + 1 < ntiles:
                      nb_ = len(nsrcs)

                      def between(cg, nsrcs=nsrcs, ndcols=ndcols, nb_=nb_, itn=it + 1):
                          if cg == 0:
                              mixer_norm_zero(itn)
                              for i in range(min(2, nb_)):
                                  norm_apply_stt(nsrcs, i)
                          else:
                              for i in (2 * cg - 2, 2 * cg - 1):
                                  if i < nb_:
                                      norm_apply_T(hT, ndcols, i)
                                      if i + 2 < nb_:
                                          norm_apply_stt(nsrcs, i + 2)
                              if cg == 3:
                                  load_g(1)
                      projT_res(aT[:, asl], [4, 4, FGB - 8], FGB, ssq=True, between=between)
                  else:
                      projT_res(aT[:, asl], [4, 4, FGB - 8], FGB, ssq=last)
              assert pi[0] == N_MEMP + N_TILEP, pi[0]

              ss_from_parts()
              act(stat[:, 16:20], stat[:, 0:4], AF.Ln, bias=epsb, scale=1.0 / D)
              act(stat[:, 24:28], stat[:, 16:20], AF.Exp, scale=-0.5)
              def final_block(tb, tb0=tb0):
                  o = xh[:, tb % 2, :]
                  stt(o, xres[:, tb, :], stat[:, 24 + tb:25 + tb], gF, ALU.mult, ALU.mult)
                  out_dmas.append(dma("sp", out_d[(tb0 + tb) * 128:(tb0 + tb + 1) * 128, :], o, sb_reads=[o]))

              if it + 1 < ntiles:
                  pending_final.extend([(final_block, tb) for tb in range(4)])
              else:
                  for tb in range(4):
                      final_block(tb)

        except _Stop:
            pass
        for d in dbg_dma_ops:
            out_dmas.append(d)
        pr.emit(es, out_dmas)
    return nc


def _F(W, m, KC):
    return np.ascontiguousarray(W[:, m * 128:(m + 1) * 128].reshape(KC, 128, 128).transpose(1, 0, 2)).reshape(128, KC * 128)


def _T(W, k0, nk, c0, width):
    return np.ascontiguousarray(W[k0 * 128:(k0 + nk) * 128, c0:c0 + width].reshape(nk, 128, width).transpose(1, 0, 2)).reshape(128, nk * width)


def prep_weights(w_in, w_attn_out, w_conv_out, w_o, w_cq, w_ckv, w_co, w_gate, w_up, w_down):
    Ws = {"w_in": w_in, "w_o": w_o, "w_cq": w_cq, "w_co": w_co, "w_gate": w_gate, "w_up": w_up, "w_down": w_down,
          "w_ck": w_ckv[:, :512], "w_cv": w_ckv[:, 512:]}
    order = mem_piece_order() + piece_order()
    wp = np.zeros((len(order), 128, 2048), np.float32)
    for i, pc in enumerate(order):
        if pc[0] == "F":
            W = Ws[pc[1]]
            a = _F(W, pc[2], W.shape[0] // 128)
        elif pc[0] == "Tv":
            a = _T(w_in, pc[1], 8, 1280, 256)
        elif pc[0] == "AOCO":
            c = pc[1]
            a = np.concatenate([_F(w_attn_out, c, 8), _F(w_conv_out, c, 8)], axis=1)
        else:
            _, name, k0, nk, cg = pc
            a = _T(Ws[name], k0, nk, cg * 512, 512)
        wp[i, :, :a.shape[1]] = a
    return wp


def prep_consts():
    inv = (1.0 / (np.float32(10000.0) ** (np.arange(0, 128, 2, dtype=np.float32) / np.float32(128)))).astype(np.float32)
    ang = np.arange(S, dtype=np.float32)[:, None] * inv[None, :]
    cos = np.cos(ang).astype(np.float32).T
    sin = np.sin(ang).astype(np.float32).T
    cs = np.zeros((2, 128, S + 256), np.float32)
    cs[0, :64, 128:128 + S] = cos; cs[0, 64:, 128:128 + S] = cos
    cs[1, :64, 128:128 + S] = sin; cs[1, 64:, 128:128 + S] = -sin
    cst = np.zeros((128, 1280), np.float32)
    cst[:, 0:128] = np.eye(128, dtype=np.float32)
    cst[:, 128:256] = 1.0
    j = np.arange(128)[:, None]; i = np.arange(128)[None, :]
    mp = np.where(j >= i, 0.0, -30000.0).astype(np.float32)
    mn = np.where(j <= i, 0.0, -30000.0).astype(np.float32)
    cst[:, 256:768] = np.tile(mp, (1, 4))
    cst[:, 768:1280] = np.tile(mn, (1, 4))
    return cs, cst


_CACHE = {}


def kernel(x, mem, g_mix, w_in, sink, conv_w, b_gate, w_attn_out, w_conv_out, w_o, g_cross, g_mem,
           w_cq, w_ckv, w_co, g_ffn, w_gate, w_up, w_down, g_final, _dbg=(), _ntiles=NT, _ncores=8, _stop=None):
    f = lambda a: np.asarray(a, dtype=np.float32)
    x = f(x); mem = f(mem)
    wp = prep_weights(f(w_in)[0], f(w_attn_out)[0], f(w_conv_out)[0], f(w_o)[0], f(w_cq)[0], f(w_ckv)[0], f(w_co)[0],
                      f(w_gate)[0], f(w_up)[0], f(w_down)[0])
    cs, cst = prep_consts()
    gv = np.stack([f(g_mix)[0], f(g_cross)[0], f(g_ffn)[0], f(g_final), f(g_mem)[0]], 0)
    small = np.zeros((128, 64), np.float32)
    cw = f(conv_w)[0]
    for j in range(8):
        for tap in range(3):
            small[:, 3 * j + tap] = cw[tap, j * 128:(j + 1) * 128]
    small[:, 24:56] = f(b_gate)[0].reshape(32, 128).T
    small[:, 56:64] = f(sink)[0][None, :]
    key = (tuple(_dbg), _ntiles, _stop)
    if key not in _CACHE:
        _CACHE[key] = build_program(_ntiles, _dbg, _stop)
    nc = _CACHE[key]
    in_maps = [{"x": np.ascontiguousarray(x[b]), "mem": np.ascontiguousarray(mem[b]), "wp": wp, "gv": gv,
                "small": small, "cs": cs, "cst": cst} for b in range(_ncores)]
    res = run_bass_kernel_spmd(nc, in_maps, core_ids=list(range(_ncores)))
    out = np.stack([r["out"] for r in res.results], 0).astype(np.float32)
    if _dbg:
        return out, res.results
    return out
```
